# Optimizing a Trainium2 kernel written in Bass

```python
import jax, jax.numpy as jnp
from jax import lax
import numpy as np

D_MODEL = 2048
BATCH = 1
SEQ = 16384
DEPTH = 1

GLA_HEADS = 4
GLA_DK = D_MODEL // 16
GLA_DV = D_MODEL // 8
GLA_GATE_RANK = 16
GLA_GATE_TAU = 16.0
GLA_CHUNK = 64
CONV_WIDTH = 4

NSA_HEADS = 8
NSA_KV_HEADS = 2
NSA_GROUP = NSA_HEADS // NSA_KV_HEADS
NSA_HEAD_DIM = 128
CMP_BLOCK = 32
CMP_STRIDE = 16
CMP_HIDDEN = 128
SEL_BLOCK = 64
SEL_TOPK = 16
WINDOW = 512
Q_BLOCK = 128

D_FF = -(-8 * D_MODEL // (3 * 256)) * 256
EPS = 1e-6
NEG = -1e30

GLA_QK = GLA_HEADS * GLA_DK
GLA_VW = GLA_HEADS * GLA_DV
GLA_QKV = 2 * GLA_QK + GLA_VW
NSA_QW = NSA_HEADS * NSA_HEAD_DIM
NSA_KVW = NSA_KV_HEADS * NSA_HEAD_DIM
MIX_WIDTH = GLA_VW + NSA_QW
PROJ_SIZES = (GLA_QKV, GLA_GATE_RANK, GLA_VW, NSA_QW, NSA_KVW, NSA_KVW, NSA_KVW, NSA_KVW, NSA_KVW, NSA_KVW, 3 * NSA_HEADS)
PROJ_WIDTH = sum(PROJ_SIZES)

kernel_name = 'hybrid_gla_nsa_block'


def rms_norm(u, g):
    uf = u.astype(jnp.float32)
    y = uf * lax.rsqrt(jnp.mean(uf * uf, axis=-1, keepdims=True) + EPS)
    return (y * g.astype(jnp.float32)).astype(u.dtype)


def causal_short_conv(u, w):
    s = u.shape[1]
    up = jnp.pad(u, ((0, 0), (CONV_WIDTH - 1, 0), (0, 0)))
    return sum(up[:, i:i + s] * w[i] for i in range(CONV_WIDTH))


def gla_mixer(q, k, v, gate_lr, out_gate, gate_w2, gate_b, norm_g):
    b_, s, _ = q.shape
    h, c = GLA_HEADS, GLA_CHUNK
    n_chunks = s // c
    log_a = jax.nn.log_sigmoid((gate_lr @ gate_w2 + gate_b).astype(jnp.float32)) / GLA_GATE_TAU

    def chunks(u, d):
        return u.reshape(b_, n_chunks, c, h, d).transpose(1, 0, 3, 2, 4)

    qc = chunks(q.astype(jnp.float32) * GLA_DK ** -0.5, GLA_DK)
    kc = chunks(k.astype(jnp.float32), GLA_DK)
    vc = chunks(v.astype(jnp.float32), GLA_DV)
    gc = chunks(log_a, GLA_DK)
    causal = jnp.tril(jnp.ones((c, c), dtype=bool))[:, :, None]

    def step(state, inp):
        qi, ki, vi, gi = inp
        b = jnp.cumsum(gi, axis=2)
        o_inter = jnp.einsum('bhtd,bhde->bhte', qi * jnp.exp(b), state)
        decay = jnp.exp(jnp.where(causal, b[:, :, :, None, :] - b[:, :, None, :, :], -jnp.inf))
        scores = jnp.einsum('bhtd,bhsd,bhtsd->bhts', qi, ki, decay)
        o_intra = jnp.einsum('bhts,bhse->bhte', scores, vi)
        b_last = b[:, :, -1:, :]
        state = (jnp.exp(b_last[:, :, 0, :])[..., None] * state
                 + jnp.einsum('bhsd,bhse->bhde', ki * jnp.exp(b_last - b), vi))
        return state, o_inter + o_intra

    state0 = jnp.zeros((b_, h, GLA_DK, GLA_DV), jnp.float32)
    _, o = lax.scan(step, state0, (qc, kc, vc, gc))
    o = o.transpose(1, 0, 3, 2, 4).reshape(b_, s, h, GLA_DV)
    o = rms_norm(o, norm_g) * jax.nn.silu(out_gate.astype(jnp.float32)).reshape(b_, s, h, GLA_DV)
    return o.reshape(b_, s, GLA_VW).astype(q.dtype)


def nsa_mixer(q, kc_raw, vc_raw, ks_raw, vs_raw, kw_raw, vw_raw, gate_logits,
              q_g, kc_g, ks_g, kw_g, k_pos, k_w1, k_w2, v_pos, v_w1, v_w2):
    b_, s, _ = q.shape
    hk, g, dh = NSA_KV_HEADS, NSA_GROUP, NSA_HEAD_DIM
    n_cmp = (s - CMP_BLOCK) // CMP_STRIDE + 1
    n_sel_blocks = s // SEL_BLOCK
    n_sel = min(SEL_TOPK, n_sel_blocks)
    scale = dh ** -0.5

    def heads(u, n):
        return u.reshape(b_, s, n, dh).transpose(0, 2, 1, 3)

    qh = rms_norm(heads(q, NSA_HEADS), q_g).reshape(b_, hk, g, s, dh)

    cmp_idx = jnp.arange(n_cmp)[:, None] * CMP_STRIDE + jnp.arange(CMP_BLOCK)[None, :]

    def compress(u, pos, w1, w2):
        blocks = u[:, :, cmp_idx] + pos
        flat = blocks.reshape(b_, hk, n_cmp, CMP_BLOCK * dh)
        return jax.nn.silu(flat @ w1) @ w2

    k_cmp = rms_norm(compress(heads(kc_raw, hk), k_pos, k_w1, k_w2), kc_g)
    v_cmp = compress(heads(vc_raw, hk), v_pos, v_w1, v_w2)
    cmp_end = jnp.arange(n_cmp) * CMP_STRIDE + CMP_BLOCK - 1

    c_start = jnp.arange(n_cmp)[:, None] * CMP_STRIDE
    s_start = jnp.arange(n_sel_blocks)[None, :] * SEL_BLOCK
    overlap = jnp.clip(jnp.minimum(c_start + CMP_BLOCK, s_start + SEL_BLOCK)
                       - jnp.maximum(c_start, s_start), 0, None).astype(jnp.float32) / CMP_STRIDE

    k_sel = rms_norm(heads(ks_raw, hk), ks_g).reshape(b_, hk, n_sel_blocks, SEL_BLOCK, dh)
    v_sel = heads(vs_raw, hk).reshape(b_, hk, n_sel_blocks, SEL_BLOCK, dh)
    win_pad = ((0, 0), (0, 0), (WINDOW, 0), (0, 0))
    k_win = jnp.pad(rms_norm(heads(kw_raw, hk), kw_g), win_pad)
    v_win = jnp.pad(heads(vw_raw, hk), win_pad)
    gates = jax.nn.sigmoid(gate_logits.astype(jnp.float32)).reshape(b_, s, hk, g, 3).transpose(0, 2, 3, 1, 4)

    gather_blocks = jax.vmap(jax.vmap(lambda blk, ix: blk[ix]))
    sel_offsets = jnp.arange(SEL_BLOCK)
    win_offsets = jnp.arange(WINDOW + Q_BLOCK)
    sel_ids = jnp.arange(n_sel_blocks)

    def query_block(s0):
        t = s0 + jnp.arange(Q_BLOCK)
        qb = lax.dynamic_slice_in_dim(qh, s0, Q_BLOCK, axis=3)

        sc = jnp.einsum('bhgqd,bhnd->bhgqn', qb, k_cmp).astype(jnp.float32) * scale
        mc = cmp_end[None, :] <= t[:, None]
        pc = jax.nn.softmax(jnp.where(mc, sc, NEG), axis=-1) * mc
        o_cmp = jnp.einsum('bhgqn,bhnd->bhgqd', pc.astype(v_cmp.dtype), v_cmp)

        imp = jnp.einsum('bhgqn,nj->bhqj', pc, overlap)
        jt = (t // SEL_BLOCK)[:, None]
        forced = (sel_ids == 0) | (sel_ids == jt) | (sel_ids == jt - 1)
        imp = jnp.where(sel_ids > jt, -jnp.inf, jnp.where(forced, jnp.inf, imp))
        _, idx = lax.top_k(imp, n_sel)
        ks = gather_blocks(k_sel, idx).reshape(b_, hk, Q_BLOCK, n_sel * SEL_BLOCK, dh)
        vs = gather_blocks(v_sel, idx).reshape(b_, hk, Q_BLOCK, n_sel * SEL_BLOCK, dh)
        kpos = (idx[..., None] * SEL_BLOCK + sel_offsets).reshape(b_, hk, Q_BLOCK, n_sel * SEL_BLOCK)
        ms = (kpos <= t[:, None])[:, :, None]
        ss = jnp.einsum('bhgqd,bhqkd->bhgqk', qb, ks).astype(jnp.float32) * scale
        ps = jax.nn.softmax(jnp.where(ms, ss, NEG), axis=-1)
        o_sel = jnp.einsum('bhgqk,bhqkd->bhgqd', ps.astype(vs.dtype), vs)

        kwb = lax.dynamic_slice_in_dim(k_win, s0, WINDOW + Q_BLOCK, axis=2)
        vwb = lax.dynamic_slice_in_dim(v_win, s0, WINDOW + Q_BLOCK, axis=2)
        wpos = s0 - WINDOW + win_offsets
        dist = t[:, None] - wpos[None, :]
        mw = (wpos[None, :] >= 0) & (dist >= 0) & (dist < WINDOW)
        sw = jnp.einsum('bhgqd,bhkd->bhgqk', qb, kwb).astype(jnp.float32) * scale
        pw = jax.nn.softmax(jnp.where(mw, sw, NEG), axis=-1)
        o_win = jnp.einsum('bhgqk,bhkd->bhgqd', pw.astype(vwb.dtype), vwb)

        gb = lax.dynamic_slice_in_dim(gates, s0, Q_BLOCK, axis=3)
        return gb[..., 0:1] * o_cmp + gb[..., 1:2] * o_sel + gb[..., 2:3] * o_win

    n_blocks = s // Q_BLOCK
    out = lax.map(query_block, jnp.arange(n_blocks) * Q_BLOCK)
    out = out.transpose(1, 0, 4, 2, 3, 5).reshape(b_, s, NSA_QW)
    return out.astype(q.dtype)


def hybrid_layer(x, attn_norm_g, w_in, gla_conv_w, gla_gate_w2, gla_gate_b, gla_norm_g,
                 nsa_q_norm_g, nsa_kc_norm_g, nsa_ks_norm_g, nsa_kw_norm_g,
                 cmp_k_pos, cmp_k_w1, cmp_k_w2, cmp_v_pos, cmp_v_w1, cmp_v_w2,
                 w_out, ffn_norm_g, w_gate, w_up, w_down):
    n = rms_norm(x, attn_norm_g)
    proj = n @ w_in
    offsets = np.cumsum(PROJ_SIZES)[:-1].tolist()
    (g_qkv, g_lr, g_out, nq, kc, vc, ks, vs, kw, vw, ng) = jnp.split(proj, offsets, axis=-1)
    g_qkv = jax.nn.silu(causal_short_conv(g_qkv, gla_conv_w))
    gq, gk, gv = jnp.split(g_qkv, [GLA_QK, 2 * GLA_QK], axis=-1)
    gla_out = gla_mixer(gq, gk, gv, g_lr, g_out, gla_gate_w2, gla_gate_b, gla_norm_g)
    nsa_out = nsa_mixer(nq, kc, vc, ks, vs, kw, vw, ng,
                        nsa_q_norm_g, nsa_kc_norm_g, nsa_ks_norm_g, nsa_kw_norm_g,
                        cmp_k_pos, cmp_k_w1, cmp_k_w2, cmp_v_pos, cmp_v_w1, cmp_v_w2)
    h = x + jnp.concatenate([gla_out, nsa_out], axis=-1) @ w_out
    m = rms_norm(h, ffn_norm_g)
    return h + (jax.nn.silu(m @ w_gate) * (m @ w_up)) @ w_down


def setup_inputs(seed: int = 0) -> dict:
    key = jax.random.key(seed)
    ks = jax.random.split(key, 22)

    def dense(k, shape, fan_in):
        return jax.random.normal(k, shape, jnp.float32) * fan_in ** -0.5

    def gain(k, d):
        return 1.0 + 0.02 * jax.random.normal(k, (DEPTH, d), jnp.float32)

    dh = NSA_HEAD_DIM
    return {
        'x': jax.random.normal(ks[0], (BATCH, SEQ, D_MODEL), jnp.float32),
        'attn_norm_g': gain(ks[1], D_MODEL),
        'w_in': dense(ks[2], (DEPTH, D_MODEL, PROJ_WIDTH), D_MODEL),
        'gla_conv_w': dense(ks[3], (DEPTH, CONV_WIDTH, GLA_QKV), CONV_WIDTH),
        'gla_gate_w2': dense(ks[4], (DEPTH, GLA_GATE_RANK, GLA_QK), GLA_GATE_RANK),
        'gla_gate_b': 0.02 * jax.random.normal(ks[5], (DEPTH, GLA_QK), jnp.float32),
        'gla_norm_g': gain(ks[6], GLA_DV),
        'nsa_q_norm_g': gain(ks[7], dh),
        'nsa_kc_norm_g': gain(ks[8], dh),
        'nsa_ks_norm_g': gain(ks[9], dh),
        'nsa_kw_norm_g': gain(ks[10], dh),
        'cmp_k_pos': 0.1 * jax.random.normal(ks[11], (DEPTH, CMP_BLOCK, dh), jnp.float32),
        'cmp_k_w1': dense(ks[12], (DEPTH, CMP_BLOCK * dh, CMP_HIDDEN), CMP_BLOCK * dh),
        'cmp_k_w2': dense(ks[13], (DEPTH, CMP_HIDDEN, dh), CMP_HIDDEN),
        'cmp_v_pos': 0.1 * jax.random.normal(ks[14], (DEPTH, CMP_BLOCK, dh), jnp.float32),
        'cmp_v_w1': dense(ks[15], (DEPTH, CMP_BLOCK * dh, CMP_HIDDEN), CMP_BLOCK * dh),
        'cmp_v_w2': dense(ks[16], (DEPTH, CMP_HIDDEN, dh), CMP_HIDDEN),
        'w_out': dense(ks[17], (DEPTH, MIX_WIDTH, D_MODEL), MIX_WIDTH),
        'ffn_norm_g': gain(ks[18], D_MODEL),
        'w_gate': dense(ks[19], (DEPTH, D_MODEL, D_FF), D_MODEL),
        'w_up': dense(ks[20], (DEPTH, D_MODEL, D_FF), D_MODEL),
        'w_down': dense(ks[21], (DEPTH, D_FF, D_MODEL), D_FF),
    }


def reference(x, attn_norm_g, w_in, gla_conv_w, gla_gate_w2, gla_gate_b, gla_norm_g,
              nsa_q_norm_g, nsa_kc_norm_g, nsa_ks_norm_g, nsa_kw_norm_g,
              cmp_k_pos, cmp_k_w1, cmp_k_w2, cmp_v_pos, cmp_v_w1, cmp_v_w2,
              w_out, ffn_norm_g, w_gate, w_up, w_down):
    for l in range(DEPTH):
        x = hybrid_layer(x, attn_norm_g[l], w_in[l], gla_conv_w[l], gla_gate_w2[l], gla_gate_b[l],
                         gla_norm_g[l], nsa_q_norm_g[l], nsa_kc_norm_g[l], nsa_ks_norm_g[l],
                         nsa_kw_norm_g[l], cmp_k_pos[l], cmp_k_w1[l], cmp_k_w2[l], cmp_v_pos[l],
                         cmp_v_w1[l], cmp_v_w2[l], w_out[l], ffn_norm_g[l], w_gate[l], w_up[l],
                         w_down[l])
    return x
```

```python
import contextlib
import numpy as np
import concourse.bass as bass
import concourse.mybir as mybir
from concourse.bass_utils import run_bass_kernel_spmd

F32 = mybir.dt.float32
BF16 = mybir.dt.bfloat16
AF = mybir.ActivationFunctionType
ALU = mybir.AluOpType
AX = mybir.AxisListType

PE, DVE, ACT, POOL, SP = 0, 1, 2, 3, 4
ENG_NAMES = ["tensor", "vector", "scalar", "gpsimd", "sync"]
N_DMA_SEMS = 12


class Buf:
    __slots__ = ("w", "r", "name", "excl")

    def __init__(self, name=""):
        self.w = None
        self.r = []
        self.name = name
        self.excl = False


class V:
    __slots__ = ("ap", "bufs")

    def __init__(self, ap, bufs):
        self.ap = ap
        self.bufs = bufs

    def __getitem__(self, idx):
        return V(self.ap[idx], self.bufs)

    def re(self, pat, **kw):
        return V(self.ap.rearrange(pat, **kw), self.bufs)

    def bc(self, shape):
        return V(self.ap.to_broadcast(list(shape)), self.bufs)


class T:
    def __init__(self, handle, name, is_dram=False):
        self.h = handle
        self.name = name
        self.buf = Buf(name)
        self.parts = {}
        self.is_dram = is_dram

    def _ap(self, idx):
        if self.is_dram:
            return self.h.ap()[idx] if idx is not None else self.h.ap()
        return self.h[idx] if idx is not None else self.h[:]

    def __getitem__(self, idx):
        return V(self._ap(idx), [self.buf])

    def p(self, key, idx=None):
        b = self.parts.get(key)
        if b is None:
            b = self.parts[key] = Buf(f"{self.name}.{key}")
        return V(self._ap(idx), [b])

    def ps(self, keys, idx=None):
        bs = []
        for key in keys:
            b = self.parts.get(key)
            if b is None:
                b = self.parts[key] = Buf(f"{self.name}.{key}")
            bs.append(b)
        return V(self._ap(idx), bs)

    def all(self, idx=None):
        return V(self._ap(idx), [self.buf] + list(self.parts.values()))

    def raw(self, ap, keys=None):
        if keys is None:
            return V(ap, [self.buf])
        return V(ap, [self.parts.setdefault(k, Buf(f"{self.name}.{k}")) for k in keys])


class TV(T):
    def __init__(self, parent, base_ap, name):
        self.h = parent.h
        self.name = name
        self.buf = parent.buf
        self.parts = parent.parts
        self.is_dram = True
        self.base = base_ap
        self.tag = name

    def _ap(self, idx):
        return self.base[idx] if idx is not None else self.base

    def p(self, key, idx=None):
        return T.p(self, (self.tag, key), idx)

    def raw(self, ap, keys=None):
        return V(ap, [self.buf] + list(self.parts.values()))


class K:
    def __init__(self, nc, es):
        self.nc = nc
        self.es = es
        self._root_es = es
        self.engs = [nc.tensor, nc.vector, nc.scalar, nc.gpsimd, nc.sync]
        self.esem = [es.enter_context(nc.semaphore(f"s_{n}")) for n in ENG_NAMES]
        self.ecnt = [0] * 5
        self.seen = [dict() for _ in range(5)]
        self.dsem = {}
        self.dcnt = {}
        self.drot = {}
        for e in (SP, ACT, POOL):
            self.dsem[e] = [es.enter_context(nc.semaphore(f"d_{ENG_NAMES[e]}_{i}")) for i in range(N_DMA_SEMS)]
            self.dcnt[e] = [0] * N_DMA_SEMS
            self.drot[e] = 0
        self.semobj = {}
        for i in range(5):
            self.semobj[("e", i)] = self.esem[i]
        for e in (SP, ACT, POOL):
            for i in range(N_DMA_SEMS):
                self.semobj[("d", e, i)] = self.dsem[e][i]
        self.csem = [es.enter_context(nc.semaphore("c_0"))]
        self.ccnt = [0]
        self.semobj[("c", 0)] = self.csem[0]
        self.n_inst = 0
        self.n_wait = 0

    def push(self):
        if not hasattr(self, "_es_stack"):
            self._es_stack = []
        self._es_stack.append(self.es)
        self.es = contextlib.ExitStack()
        self.es.__enter__()

    def pop(self):
        self.barrier()
        self.es.__exit__(None, None, None)
        self.es = self._es_stack.pop()

    def barrier(self):
        for a in range(5):
            deps = {}
            for e in range(5):
                if e != a and self.ecnt[e] > 0:
                    self._need(a, deps, ("e", e), self.ecnt[e])
            if a != PE and self.ecnt[a] > 0:
                self._need(a, deps, ("e", a), self.ecnt[a])
            for q in (SP, ACT, POOL):
                for i in range(N_DMA_SEMS):
                    if self.dcnt[q][i] > 0:
                        self._need(a, deps, ("d", q, i), self.dcnt[q][i])
            self._emit_waits(a, deps)

    def collective(self, kind, ins, outs, slot=0):
        a = POOL
        slot = 0
        sk = ("c", slot)
        deps = self._collect(a, ins, outs)
        self._emit_waits(a, deps)
        inst = self.nc.gpsimd.collective_compute(kind, ALU.bypass, replica_groups=[list(range(8))],
                                                 ins=[v.ap for v in ins], outs=[v.ap for v in outs])
        self.ccnt[slot] += 1
        inst.then_inc(self.csem[slot], 1)
        self.nc.gpsimd.wait_ge(self.csem[slot], self.ccnt[slot])
        if not hasattr(self, "_cc_dummy"):
            self._cc_dummy = (T(self.nc.dram_tensor("cc_dummy_a", [16, 64], F32), "cc_dummy_a", is_dram=True),
                              T(self.nc.dram_tensor("cc_dummy_b", [16, 64], F32), "cc_dummy_b", is_dram=True))
        da, db = self._cc_dummy
        obufs = [b for v in outs for b in v.bufs]
        self.dma(POOL, V(db.h.ap(), db[:].bufs + obufs), V(da.h.ap(), da[:].bufs + [b for v in ins for b in v.bufs]))
        return inst

    def sbuf(self, name, shape, dtype):
        return T(self.es.enter_context(self.nc.sbuf_tensor(name, list(shape), dtype)), name)

    def psum(self, name, shape, dtype):
        t = T(self.es.enter_context(self.nc.psum_tensor(name, list(shape), dtype)), name)
        t.buf.excl = True
        return t

    def dram(self, name, shape, dtype, kind=None):
        if kind is None:
            h = self.nc.dram_tensor(name, list(shape), dtype)
        else:
            h = self.nc.dram_tensor(name, list(shape), dtype, kind=kind)
        return T(h, name, is_dram=True)

    def _need(self, a, deps, semkey, val):
        if a == PE and semkey == ("e", PE):
            return
        if val <= self.seen[a].get(semkey, 0):
            return
        if deps.get(semkey, 0) < val:
            deps[semkey] = val

    def _collect(self, a, reads, writes):
        deps = {}
        for v in reads:
            for b in v.bufs:
                if b.w is not None:
                    self._need(a, deps, b.w[0], b.w[1])
                if b.excl:
                    for (sk, val) in b.r:
                        if sk != ("e", a):
                            self._need(a, deps, sk, val)
        for v in writes:
            for b in v.bufs:
                if b.w is not None:
                    self._need(a, deps, b.w[0], b.w[1])
                for (sk, val) in b.r:
                    self._need(a, deps, sk, val)
        return deps

    def _emit_waits(self, a, deps):
        eng = self.engs[a]
        for sk, val in deps.items():
            eng.wait_ge(self.semobj[sk], val)
            self.seen[a][sk] = val
            self.n_wait += 1

    def _mark(self, reads, writes, sk, val):
        for v in reads:
            for b in v.bufs:
                b.r = [(k2, v2) for (k2, v2) in b.r if k2 != sk]
                b.r.append((sk, val))
        for v in writes:
            for b in v.bufs:
                b.w = (sk, val)
                b.r = []

    def op(self, a, fn, reads, writes):
        deps = self._collect(a, reads, writes)
        self._emit_waits(a, deps)
        inst = fn()
        self.ecnt[a] += 1
        inst.then_inc(self.esem[a], 1)
        sk = ("e", a)
        self._mark(reads, writes, sk, self.ecnt[a])
        self.n_inst += 1
        return inst

    def dma(self, a, out, in_, **kw):
        i = self.drot[a]
        self.drot[a] = (i + 1) % N_DMA_SEMS
        sk = ("d", a, i)
        deps = self._collect(a, [in_], [out])
        if self.dcnt[a][i] > 0:
            self._need(a, deps, sk, self.dcnt[a][i])
        self._emit_waits(a, deps)
        inst = self.engs[a].dma_start(out=out.ap, in_=in_.ap, **kw)
        self.dcnt[a][i] += 16
        inst.then_inc(self.dsem[a][i], 16)
        self._mark([in_], [out], sk, self.dcnt[a][i])
        self.n_inst += 1
        return inst

    def wait_all(self, a, views):
        deps = self._collect(a, views, [])
        self._emit_waits(a, deps)

    def mm(self, out, lhsT, rhs, start=True, stop=True, **kw):
        return self.op(PE, lambda: self.nc.tensor.matmul(out.ap, lhsT.ap, rhs.ap, start=start, stop=stop, **kw),
                       [lhsT, rhs], [out])

    def transpose(self, out, in_, ident):
        return self.op(PE, lambda: self.nc.tensor.transpose(out.ap, in_.ap, ident.ap), [in_, ident], [out])

    def act(self, out, in_, func, bias=None, scale=1.0, accum_out=None, eng=ACT):
        reads = [in_]
        kw = {}
        if bias is not None:
            if isinstance(bias, V):
                reads.append(bias)
                kw["bias"] = bias.ap
            else:
                kw["bias"] = bias
        if isinstance(scale, V):
            reads.append(scale)
            kw["scale"] = scale.ap
        else:
            kw["scale"] = scale
        writes = [out]
        if accum_out is not None:
            writes.append(accum_out)
            kw["accum_out"] = accum_out.ap
        return self.op(ACT, lambda: self.nc.scalar.activation(out=out.ap, in_=in_.ap, func=func, **kw), reads, writes)

    def tt(self, out, in0, in1, op, eng=DVE):
        e = self.engs[eng]
        return self.op(eng, lambda: e.tensor_tensor(out=out.ap, in0=in0.ap, in1=in1.ap, op=op), [in0, in1], [out])

    def ts(self, out, in0, s1, op0, s2=None, op1=None, eng=DVE, accum_out=None):
        e = self.engs[eng]
        reads = [in0]
        a1 = s1
        a2 = s2
        if isinstance(s1, V):
            reads.append(s1)
            a1 = s1.ap
        if isinstance(s2, V):
            reads.append(s2)
            a2 = s2.ap
        kw = {}
        writes = [out]
        if op1 is not None:
            kw["op1"] = op1
        if accum_out is not None:
            kw["accum_out"] = accum_out.ap
            writes.append(accum_out)
        return self.op(eng, lambda: e.tensor_scalar(out=out.ap, in0=in0.ap, scalar1=a1, scalar2=a2, op0=op0, **kw),
                       reads, writes)

    def stt(self, out, in0, scalar, in1, op0, op1, accum_out=None):
        reads = [in0, in1]
        sc = scalar
        if isinstance(scalar, V):
            reads.append(scalar)
            sc = scalar.ap
        kw = {}
        writes = [out]
        if accum_out is not None:
            kw["accum_out"] = accum_out.ap
            writes.append(accum_out)
        return self.op(DVE, lambda: self.nc.vector.scalar_tensor_tensor(out=out.ap, in0=in0.ap, scalar=sc, in1=in1.ap,
                                                                        op0=op0, op1=op1, **kw), reads, writes)

    def copy(self, out, in_, eng=DVE):
        if eng == ACT:
            return self.op(ACT, lambda: self.nc.scalar.copy(out=out.ap, in_=in_.ap), [in_], [out])
        e = self.engs[eng]
        return self.op(eng, lambda: e.tensor_copy(out=out.ap, in_=in_.ap), [in_], [out])

    def memset(self, out, val, eng=DVE):
        e = self.engs[eng]
        return self.op(eng, lambda: e.memset(out.ap, val), [], [out])

    def reduce(self, out, in_, op, axis=AX.X, eng=DVE):
        e = self.engs[eng]
        return self.op(eng, lambda: e.tensor_reduce(out=out.ap, in_=in_.ap, op=op, axis=axis), [in_], [out])

    def recip(self, out, in_):
        return self.op(DVE, lambda: self.nc.vector.reciprocal(out=out.ap, in_=in_.ap), [in_], [out])

NCORES = 8
GLA_SEGS = 32
GLA_STOP = 99
QB = SP
SEQ = 16384
SL = 2048
NTL = 16
DM = 2048
DFF = 5632
EPS = 1e-6
O_GQKV, O_GLR, O_GOUT, O_NQ, O_KC, O_VC, O_KS, O_VS, O_KW, O_VW, O_NG, PW = (
    0, 2048, 2064, 3088, 4112, 4368, 4624, 4880, 5136, 5392, 5648, 5672)
R_KC, R_VC, R_KS, R_KW, AGC_ROWS = 0, 256, 512, 768, 1024


def phase1(k, io, sc):
    nc = k.nc
    k.push()
    ident = k.sbuf("p1_ident", [128, 128], BF16)
    k.dma(POOL, ident[:], io["ident"][:])
    ones = k.sbuf("p1_ones", [128, 128], BF16)
    k.memset(ones[:], 1.0)
    grep = k.sbuf("p1_grep", [128, DM], F32)
    k.dma(SP, grep[:], io["attn_norm_g"][:].bc([128, DM]))
    gvec = k.sbuf("p1_gvec", [128, 3], F32)
    for i, nm in enumerate(["nsa_q_norm_g", "nsa_ks_norm_g", "nsa_kw_norm_g"]):
        k.dma(SP, gvec[:, i:i + 1], io[nm][:].re("o d -> d o"))
    nT = k.sbuf("p1_nT", [128, 16, SL], BF16)
    xt = [k.sbuf(f"p1_x{i}", [128, DM], F32) for i in range(2)]
    nt = [k.sbuf(f"p1_n{i}", [128, DM], BF16) for i in range(2)]
    junk = k.sbuf("p1_junk", [128, DM], F32)
    ss = k.sbuf("p1_ss", [128, 16], F32)
    t1 = k.sbuf("p1_t1", [128, 16], F32)
    t2 = k.sbuf("p1_t2", [128, 16], F32)
    rstd = k.sbuf("p1_rstd", [128, 16], F32)
    pT = [k.psum(f"p1_pT{i}", [128, 1024], BF16) for i in range(2)]
    x = io["x"]
    for j in range(NTL):
        xj = xt[j % 2]
        nj = nt[j % 2]
        k.dma(SP, xj[:], x.p(j, (slice(j * 128, (j + 1) * 128), slice(None))))
        k.stt(junk[:], xj[:], 1.0, xj[:], ALU.mult, ALU.mult, accum_out=ss.p(j, (slice(None), slice(j, j + 1))))
        k.ts(t1.p(j, (slice(None), slice(j, j + 1))), ss.p(j, (slice(None), slice(j, j + 1))), 1.0 / DM, ALU.mult, EPS, ALU.add)
        k.act(t2.p(j, (slice(None), slice(j, j + 1))), t1.p(j, (slice(None), slice(j, j + 1))), AF.Ln)
        k.act(rstd.p(j, (slice(None), slice(j, j + 1))), t2.p(j, (slice(None), slice(j, j + 1))), AF.Exp, scale=-0.5)
        k.stt(nj[:], xj[:], rstd.p(j, (slice(None), slice(j, j + 1))), grep[:], ALU.mult, ALU.mult)
        for half in range(2):
            for q in range(8):
                c = half * 8 + q
                k.transpose(pT[half][:, q * 128:(q + 1) * 128], nj[:, c * 128:(c + 1) * 128], ident[:])
            k.copy(nT.p(j, (slice(None), slice(half * 8, half * 8 + 8), slice(j * 128, (j + 1) * 128))),
                   pT[half][:].re("p (a b) -> p a b", a=8), eng=ACT)

    wt = [k.sbuf(f"p1_w{i}", [128, 16, 512], BF16) for i in range(2)]
    win_v = io["w_in"].h.ap().rearrange("(kc p) n -> p kc n", p=128)
    psA = [k.psum(f"p1_psA{i}", [128, 512], F32) for i in range(4)]
    psB = [k.psum(f"p1_psB{i}", [128, 512], F32) for i in range(2)]
    stage_c = [k.sbuf(f"p1_sc{i}", [128, SL], BF16) for i in range(2)]
    stage_t = [k.sbuf(f"p1_st{i}", [128, 512], F32) for i in range(2)]
    stage_tb = [k.sbuf(f"p1_stb{i}", [128, 512], BF16) for i in range(2)]
    sqb = [k.sbuf(f"p1_sqb{i}", [128, 512], BF16) for i in range(2)]
    lnb = [k.sbuf(f"p1_lnb{i}", [128, 512], F32) for i in range(2)]
    rsb = [k.sbuf(f"p1_rsb{i}", [128, 512], F32) for i in range(2)]
    cnt = {"w": 0, "ps": 0, "sc": 0, "st": 0, "nb": 0}

    def load_w(c0, width):
        w = wt[cnt["w"] % 2]
        cnt["w"] += 1
        k.dma(POOL, w[:, :, 0:width], io["w_in"].raw(win_v[:, :, c0:c0 + width]))
        return w

    def chan_block(c0, width, dest, drow0, norm_g=None):
        w = load_w(c0, width)
        for sb in range((width + 127) // 128):
            m = min(128, width - sb * 128)
            st = stage_c[cnt["sc"] % 2]
            cnt["sc"] += 1
            for tg in range(4):
                ps = psA[cnt["ps"] % 4]
                cnt["ps"] += 1
                rhs_keys = list(range(4 * tg, 4 * tg + 4))
                for kc in range(16):
                    k.mm(ps[0:m, :], w[:, kc, sb * 128:sb * 128 + m],
                         nT.ps(rhs_keys, (slice(None), kc, slice(tg * 512, (tg + 1) * 512))),
                         start=(kc == 0), stop=(kc == 15))
                dst = st[0:m, tg * 512:(tg + 1) * 512]
                if norm_g is None:
                    k.copy(dst, ps[0:m, :], eng=(ACT if tg % 2 == 0 else DVE))
                else:
                    i = cnt["nb"] % 2
                    cnt["nb"] += 1
                    k.act(sqb[i][:], ps[:], AF.Square)
                    p2 = psB[i]
                    k.mm(p2[:], ones[:], sqb[i][:])
                    k.act(lnb[i][:], p2[:], AF.Ln, scale=1.0 / 128, bias=epsb[:])
                    k.act(rsb[i][:], lnb[i][:], AF.Exp, scale=-0.5)
                    k.stt(dst, ps[:], gvec[:, norm_g:norm_g + 1], rsb[i][:], ALU.mult, ALU.mult)
            r0 = drow0 + sb * 128
            k.dma(SP, dest.p(("r", r0), (slice(r0, r0 + m), slice(None))), st[0:m, :])

    def tok_block(c0, width, dest, dcol0, func, out_bf16):
        w = load_w(c0, width)
        for j in range(NTL):
            ps = psA[cnt["ps"] % 4]
            cnt["ps"] += 1
            for kc in range(16):
                k.mm(ps[:, 0:width], nT.p(j, (slice(None), kc, slice(j * 128, (j + 1) * 128))), w[:, kc, 0:width],
                     start=(kc == 0), stop=(kc == 15))
            st = (stage_tb if out_bf16 else stage_t)[cnt["st"] % 2]
            cnt["st"] += 1
            if func is None:
                k.copy(st[:, 0:width], ps[:, 0:width], eng=(ACT if j % 2 == 0 else DVE))
            else:
                k.act(st[:, 0:width], ps[:, 0:width], func)
            k.dma(SP, dest.p(("t", j, dcol0), (slice(j * 128, (j + 1) * 128), slice(dcol0, dcol0 + width))), st[:, 0:width])

    epsb = k.sbuf("p1_eps", [128, 1], F32)
    k.memset(epsb[:], EPS)
    agc = sc["agc_in"]
    for b in range(2):
        tok_block(O_GOUT + 512 * b, 512, sc["gout"], 512 * b, AF.Silu, False)
    for b in range(2):
        chan_block(O_NQ + 512 * b, 512, sc["qn"], 512 * b, norm_g=0)
    chan_block(O_KC, 256, agc, R_KC)
    chan_block(O_VC, 256, agc, R_VC)
    chan_block(O_KS, 256, agc, R_KS, norm_g=1)
    tok_block(O_VS, 256, sc["agv_in"], 0, None, True)
    chan_block(O_KW, 256, agc, R_KW, norm_g=2)
    tok_block(O_VW, 256, sc["agv_in"], 256, None, True)
    tok_block(O_NG, 24, sc["ng"], 0, AF.Sigmoid, False)
    k.pop()


def phase_gla(k, io, sc):
    k.push()
    W = k.sbuf("g_W", [128, 16, 400], BF16)
    k.dma(POOL, W[:], io["gla_w"].raw(io["gla_w"].h.ap().rearrange("(kc p) n -> p kc n", p=128)))
    cw = k.sbuf("g_cw", [128, 12], F32)
    k.dma(SP, cw[:], io["gla_cw"][:])
    w2a = k.sbuf("g_w2a", [17, 128], F32)
    k.dma(SP, w2a[:], io["gla_w2a"][:])
    tris = k.sbuf("g_tris", [128, 128], F32)
    k.dma(SP, tris[:], io["c_tris"][:])
    m2 = k.sbuf("g_m2", [128, 128], F32)
    k.dma(SP, m2[:], io["c_m2"][:])
    ident = k.sbuf("g_ident", [128, 128], BF16)
    k.dma(POOL, ident[:], io["ident"][:])
    S = k.sbuf("g_S", [128, 128], F32)
    k.memset(S[:], 0.0)
    Sb = k.sbuf("g_Sb", [128, 128], BF16)
    k.memset(Sb[:], 0.0)
    U = [k.sbuf(f"g_U{g}", [128, 515], F32) for g in range(3)]
    for g in range(3):
        k.memset(U[g][:], 0.0)
    lra = k.sbuf("g_lra", [17, 512], F32)
    k.memset(lra[:], 1.0)
    oneb = k.sbuf("g_oneb", [128, 1], F32)
    k.memset(oneb[:], 1.0)
    seg = [k.sbuf(f"g_seg{i}", [128, 4, 16, 128], BF16) for i in range(2)]
    Cq = [k.sbuf(f"g_C{g}", [128, 512], F32) for g in range(3)]
    acc = [k.sbuf(f"g_acc{g}", [128, 512], F32) for g in range(2)]
    psP = [k.psum(f"g_psP{i}", [128, 512], F32) for i in range(2)]
    psE = k.psum("g_psE", [128, 512], F32)
    psO = k.psum("g_psO", [128, 512], F32)
    psD = k.psum("g_psD", [128, 512], F32)
    psT = k.psum("g_psT", [128, 1024], BF16)
    e1 = k.sbuf("g_e1", [128, 128], F32)
    lg = k.sbuf("g_lg", [128, 128], F32)
    bsb = k.sbuf("g_bsb", [128, 128], F32)
    eb = k.sbuf("g_eb", [128, 128], F32)
    enb = k.sbuf("g_enb", [128, 128], F32)
    ebl = k.sbuf("g_ebl", [128, 128], F32)
    nbl = k.sbuf("g_nbl", [128, 2], F32)
    qt = k.sbuf("g_qt", [128, 128], BF16)
    kt = k.sbuf("g_kt", [128, 128], BF16)
    kh = k.sbuf("g_kh", [128, 128], BF16)
    vb = k.sbuf("g_vb", [128, 128], BF16)
    khT = k.sbuf("g_khT", [128, 128], BF16)
    vT = k.sbuf("g_vT", [128, 128], BF16)
    A = k.sbuf("g_A", [128, 128], BF16)
    ost = [k.sbuf(f"g_ost{i}", [128, 4, 128], BF16) for i in range(2)]
    grep = k.sbuf("g_grep", [128, DM], F32)
    k.dma(SP, grep[:], io["attn_norm_g"][:].bc([128, DM]))
    xt = [k.sbuf(f"g_x{i}", [128, DM], F32) for i in range(2)]
    nt = [k.sbuf(f"g_n{i}", [128, DM], BF16) for i in range(2)]
    junk = k.sbuf("g_junk", [128, DM], F32)
    st3 = k.sbuf("g_st3", [128, 4], F32)
    blocks = [(0, 128), (128, 128), (256, 128), (384, 16)]
    tcount = 0
    pcount = 0
    for s in range(GLA_SEGS):
        j, r0 = s // 2, 4 * (s % 2)
        sg = seg[s % 2]
        for r in range(4):
            T_ = 4 * s + r
            xj, nj = xt[tcount % 2], nt[tcount % 2]
            tcount += 1
            k.dma(SP, xj[:], io["x_full"].all((slice(T_ * 128, (T_ + 1) * 128), slice(None))))
            k.stt(junk[:], xj[:], 1.0, xj[:], ALU.mult, ALU.mult, accum_out=st3[:, 0:1])
            k.ts(st3[:, 1:2], st3[:, 0:1], 1.0 / DM, ALU.mult, EPS, ALU.add)
            k.act(st3[:, 2:3], st3[:, 1:2], AF.Ln)
            k.act(st3[:, 3:4], st3[:, 2:3], AF.Exp, scale=-0.5)
            k.stt(nj[:], xj[:], st3[:, 3:4], grep[:], ALU.mult, ALU.mult)
            for half in range(2):
                for q in range(8):
                    c_ = half * 8 + q
                    k.transpose(psT[:, q * 128:(q + 1) * 128], nj[:, c_ * 128:(c_ + 1) * 128], ident[:])
                k.copy(sg[:, r, half * 8:half * 8 + 8, :], psT[:].re("p (a b) -> p a b", a=8), eng=ACT)
        if GLA_STOP <= 2:
            continue
        for g, (c0, m) in enumerate(blocks):
            ps = psP[pcount % 2]
            pcount += 1
            for kc in range(16):
                k.mm(ps[0:m, :], W[:, kc, c0:c0 + m], sg[:, :, kc, :], start=(kc == 0), stop=(kc == 15))
            if g < 3:
                k.copy(U[g][:, 3:515], ps[:, :], eng=ACT)
            else:
                k.copy(lra[0:16, :], ps[0:16, :], eng=DVE)
        if GLA_STOP <= 3:
            continue
        for g in range(3):
            a_ = acc[g % 2]
            k.ts(a_[:], U[g][:, 0:512], cw[:, 4 * g:4 * g + 1], ALU.mult)
            for i in range(1, 4):
                k.stt(a_[:], U[g][:, i:i + 512], cw[:, 4 * g + i:4 * g + i + 1], a_[:], ALU.mult, ALU.add)
            k.act(Cq[g][:], a_[:], AF.Silu)
            k.copy(U[g][:, 0:3], U[g][:, 512:515], eng=DVE)
        os_ = ost[s % 2]
        if GLA_STOP <= 4:
            continue
        for pr in range(4):
            cs = slice(pr * 128, (pr + 1) * 128)
            k.mm(psE[:, 0:128], lra[0:17, cs], w2a[0:17, :])
            k.act(e1[:], psE[:, 0:128], AF.Exp, scale=-1.0)
            k.act(lg[:], e1[:], AF.Ln, bias=oneb[:])
            if GLA_STOP <= 5:
                continue
            k.mm(psE[:, 128:256], lg[:], tris[:])
            k.copy(bsb[:], psE[:, 128:256], eng=DVE)
            k.act(eb[:], bsb[:], AF.Exp)
            k.act(enb[:], bsb[:], AF.Exp, scale=-1.0)
            for c in range(2):
                k.act(ebl[:, c * 64:(c + 1) * 64], bsb[:, c * 64:(c + 1) * 64], AF.Exp, scale=-1.0,
                      bias=bsb[:, c * 64 + 63:c * 64 + 64])
            if GLA_STOP <= 6:
                continue
            k.stt(qt[:], Cq[0][:, cs], 128 ** -0.5, eb[:], ALU.mult, ALU.mult)
            k.tt(kt[:], Cq[1][:, cs], enb[:], ALU.mult)
            k.tt(kh[:], Cq[1][:, cs], ebl[:], ALU.mult)
            k.copy(vb[:], Cq[2][:, cs], eng=ACT)
            k.transpose(psT[:, 0:128], kh[:], ident[:])
            k.transpose(psT[:, 128:256], vb[:], ident[:])
            k.copy(khT[:], psT[:, 0:128], eng=ACT)
            k.copy(vT[:], psT[:, 128:256], eng=DVE)
            if GLA_STOP <= 7:
                continue
            k.mm(psE[:, 256:384], kt[:], qt[:])
            k.tt(A[:], psE[:, 256:384], m2[:], ALU.mult)
            if GLA_STOP <= 8:
                continue
            k.mm(psO[:, 0:128], A[:], vT[:], start=True, stop=False)
            for c in range(2):
                ps_ = slice(c * 64, (c + 1) * 64)
                k.mm(psO[ps_, 0:128], qt[:, ps_], Sb[:], start=False, stop=(c == 1))
                k.mm(psD[:, c * 128:(c + 1) * 128], khT[ps_, :], vT[ps_, :])
                k.stt(S[:], S[:], eb[:, c * 64 + 63:c * 64 + 64], psD[:, c * 128:(c + 1) * 128], ALU.mult, ALU.add)
                k.copy(Sb[:], S[:], eng=ACT)
            k.copy(os_[:, pr, :], psO[:, 0:128], eng=ACT)
        if GLA_STOP <= 9:
            continue
        dst = sc["gla_in"].base[s * 512:(s + 1) * 512, :].rearrange("(a p) e -> p a e", p=128)
        k.dma(SP, sc["gla_in"].raw(dst, keys=[s]), os_[:])
    k.pop()
    if not sc.get("_skip_ag"):
        k.collective("AllGather", [sc["agx_in"].all()], [sc["agx_out"].all()], slot=0)
        k.barrier()


def norm_chan(k, ps, dst, ones, gcol, epsb, tmp):
    sqb, p2, lnb, rsb = tmp
    n = ps.ap.shape[-1]
    k.act(sqb[:, 0:n], ps, AF.Square)
    k.mm(p2[:, 0:n], ones[:], sqb[:, 0:n])
    k.act(lnb[:, 0:n], p2[:, 0:n], AF.Ln, scale=1.0 / 128, bias=epsb[:])
    k.act(rsb[:, 0:n], lnb[:, 0:n], AF.Exp, scale=-0.5)
    k.stt(dst, ps, gcol, rsb[:, 0:n], ALU.mult, ALU.mult)


def phase_nsa(k, io, sc):
    SCALE = 128 ** -0.5
    k.push()
    agc, agv = sc["agc_out"], sc["agv_out"]
    agc_ap, agv_ap = agc.base, agv.base
    ident = k.sbuf("n_ident", [128, 128], BF16)
    k.dma(POOL, ident[:], io["ident"][:])
    ones = k.sbuf("n_ones", [128, 128], BF16)
    k.memset(ones[:], 1.0)
    epsb = k.sbuf("n_eps", [128, 1], F32)
    k.memset(epsb[:], EPS)
    gkc = k.sbuf("n_gkc", [128, 1], F32)
    k.dma(SP, gkc[:], io["nsa_kc_norm_g"][:].re("o d -> d o"))
    Kcn = k.sbuf("n_Kcn", [128, 2, 1024], BF16)
    k.memset(Kcn[:], 0.0)
    Vc = k.sbuf("n_Vc", [128, 8, 2, 128], BF16)
    k.memset(Vc[:], 0.0)
    psS = [k.psum(f"n_psS{i}", [128, 512], F32) for i in range(2)]
    psM = k.psum("n_psM", [128, 512], F32)
    psT = k.psum("n_psT", [128, 1024], BF16)
    psA = [k.psum(f"n_psA{i}", [128, 512], F32) for i in range(4)]
    k.push()
    Gb = k.sbuf("c_G", [128, 16, 8, 128], BF16)
    w1 = k.sbuf("c_w1", [128, 32, 128], BF16)
    w2 = k.sbuf("c_w2", [128, 128], BF16)
    posT = k.sbuf("c_posT", [128, 32], BF16)
    posb = k.sbuf("c_posb", [128, 1], F32)
    hT = k.sbuf("c_hT", [128, 512], BF16)
    tmp = (k.sbuf("c_sqb", [128, 512], BF16), psM, k.sbuf("c_lnb", [128, 512], F32), k.sbuf("c_rsb", [128, 512], F32))
    for which in range(2):
        nm = "cmp_k" if which == 0 else "cmp_v"
        k.dma(POOL, w1[:], io[nm + "_w1"].raw(io[nm + "_w1"].h.ap().rearrange("(l p) h -> p l h", p=128)))
        k.dma(POOL, w2[:], io[nm + "_w2"][:])
        with k.nc.allow_non_contiguous_dma(reason="tiny transposed load"):
            k.dma(POOL, posT[:], io[nm + "_pos"].raw(io[nm + "_pos"].h.ap().rearrange("l d -> d l")))
        for l in range(32):
            k.mm(psM[:, 0:1], w1[:, l, :], posT[:, l:l + 1], start=(l == 0), stop=(l == 31))
        k.copy(posb[:], psM[:, 0:1], eng=DVE)
        for kvh in range(2):
            row0 = (R_KC if which == 0 else R_VC) + kvh * 128
            for r in range(8):
                src = agc_ap[r, row0:row0 + 128, :].rearrange("p (j t) -> p j t", t=128)
                k.dma(SP if r % 2 == 0 else QB, Gb[:, :, r, :], agc.raw(src))
            Gf = Gb[:].re("p j r t -> p (j r t)")
            for nb in range(2):
                n0, cnt_ = 512 * nb, (512 if nb == 0 else 511)
                ps = psS[nb]
                for l in range(32):
                    k.mm(ps[:, 0:cnt_], w1[:, l, :], Gf[:, l + 16 * n0:min(16384, l + 16 * (n0 + cnt_)):16],
                         start=(l == 0), stop=(l == 31))
                k.act(hT[:, 0:cnt_], ps[:, 0:cnt_], AF.Silu, bias=posb[:])
                if which == 0:
                    k.mm(psA[0][:, 0:cnt_], w2[:], hT[:, 0:cnt_])
                    norm_chan(k, psA[0][:, 0:cnt_], Kcn[:, kvh, n0:n0 + cnt_], ones, gkc[:, 0:1], epsb, tmp)
                else:
                    for sub in range(4):
                        m = min(128, cnt_ - sub * 128)
                        k.mm(psA[1][0:m, sub * 128:(sub + 1) * 128], hT[:, sub * 128:sub * 128 + m], w2[:])
                        k.copy(Vc[0:m, nb * 4 + sub, kvh, :], psA[1][0:m, sub * 128:(sub + 1) * 128],
                               eng=(ACT if sub % 2 else DVE))
    k.pop()
    Gx = k.sbuf("n_Gx", [128, 8192], BF16)
    for q4 in range(4):
        k.dma(POOL, Gx[:, q4 * 2048:(q4 + 1) * 2048], io["c_gx"].all((slice(None), slice(q4 * 2048, (q4 + 1) * 2048))))
    cmpM = k.sbuf("n_cmpM", [128, 128], F32)
    k.dma(SP, cmpM[:], io["c_cmpm"][:])
    FA = k.sbuf("n_FA", [128, 17], F32)
    k.dma(SP, FA[:], io["c_fa"][:])
    VA = k.sbuf("n_VA", [128, 17], F32)
    k.dma(SP, VA[:], io["c_va"][:])
    CMT = k.sbuf("n_CMT", [128, 8, 128], BF16)
    k.dma(POOL, CMT[:], io["c_cmt"].raw(io["c_cmt"].h.ap().rearrange("p (m q) -> p m q", m=8)))
    WMT = k.sbuf("n_WMT", [128, 12, 128], BF16)
    k.dma(POOL, WMT[:], io["c_wmt"].raw(io["c_wmt"].h.ap().rearrange("p (m q) -> p m q", m=12)))
    Qt = [k.sbuf(f"n_Qt{i}", [128, 8, 128], BF16) for i in range(2)]
    ngt = [k.sbuf(f"n_ng{i}", [128, 24], F32) for i in range(2)]
    Ecmp = [k.sbuf(f"n_E{i}", [128, 1024], F32) for i in range(2)]
    Pb = [k.sbuf(f"n_Pb{i}", [128, 1024], BF16) for i in range(2)]
    PbT = [k.sbuf(f"n_PbT{i}", [128, 128], BF16) for i in range(2)]
    Pacc = k.sbuf("n_Pacc", [128, 1024], F32)
    den = k.sbuf("n_den", [128, 8], F32)
    rden = k.sbuf("n_rden", [128, 8], F32)
    imp = k.sbuf("n_imp", [128, 256], F32)
    imp2 = k.sbuf("n_imp2", [128, 256], F32)
    mx = k.sbuf("n_mx", [128, 16], F32)
    selb = k.sbuf("n_sel", [128, 256], BF16)
    k.memset(selb[:], 0.0)
    selT = [k.sbuf(f"n_selT{i}", [128, 128], BF16) for i in range(2)]
    KT = [k.sbuf(f"n_KT{i}", [128, 12, 128], BF16) for i in range(2)]
    Vt = [k.sbuf(f"n_Vt{i}", [128, 12, 129], BF16) for i in range(2)]
    for i in range(2):
        k.memset(Vt[i][:], 1.0)
    Ee = [k.sbuf(f"n_Ee{i}", [128, 4, 128], BF16) for i in range(2)]
    Pm = [k.sbuf(f"n_Pm{i}", [128, 4, 128], BF16) for i in range(2)]
    MT = [k.sbuf(f"n_MT{i}", [128, 128], BF16) for i in range(2)]
    nacc = k.sbuf("n_acc", [128, 1024], F32)
    nob = [k.sbuf(f"n_ob{i}", [128, 1024], BF16) for i in range(2)]
    fsc = k.sbuf("n_fsc", [128, 8], F32)
    cnt = {"kv": 0, "u": 0, "pb": 0}

    def run_units(qt, kvh, units):
        state = {}

        def front(i):
            un = units[i]
            if un.get("pre") is not None:
                un["pre"]()
            u = cnt["u"]
            cnt["u"] += 1
            ps = psS[u % 2]
            k.mm(ps[:], un["kt"](), qt[:, kvh * 4:(kvh + 1) * 4, :])
            e = Ee[u % 2]
            k.act(e[:].re("p g q -> p (g q)"), ps[:], AF.Exp, scale=SCALE)
            p = Pm[u % 2]
            un["mask"](p, e, u)
            state[i] = p

        def back(i):
            un = units[i]
            p = state.pop(i)
            vt_v = un["vt"]()
            for g in range(4):
                k.mm(psA[g][:, 0:129], p[:, g, :], vt_v, start=(i == 0), stop=(i == len(units) - 1))

        front(0)
        for i in range(len(units)):
            if i + 1 < len(units):
                front(i + 1)
            back(i)

    def finalize(kvh, br, ng, init):
        for g in range(4):
            hh = kvh * 4 + g
            k.recip(fsc[:, hh:hh + 1], psA[g][:, 128:129])
            k.tt(fsc[:, hh:hh + 1], fsc[:, hh:hh + 1], ng[:, hh * 3 + br:hh * 3 + br + 1], ALU.mult)
            dst = nacc[:, hh * 128:(hh + 1) * 128]
            k.stt(dst, psA[g][:, 0:128], fsc[:, hh:hh + 1], dst, ALU.mult, ALU.add)

    for j in range(NTL):
        qt, ng = Qt[j % 2], ngt[j % 2]
        k.dma(SP, qt[:], sc["qn"].raw(sc["qn"].h.ap()[:, j * 128:(j + 1) * 128].rearrange("(h p) t -> p h t", p=128)))
        k.dma(SP, ng[:], sc["ng"].all((slice(j * 128, (j + 1) * 128), slice(None))))
        ncols = 64 * (j + 1)
        nblk = 16 * (j + 1)
        nch = (ncols + 127) // 128
        lo = max(0, 64 * j - 64)
        ulo = lo - (64 * j - 64)
        for hh in range(8):
            kvh, g = hh // 4, hh % 4
            E = Ecmp[hh % 2]
            for c0 in range(0, ncols, 512):
                w = min(512, ncols - c0)
                k.mm(psS[c0 // 512][:, 0:w], qt[:, hh, :], Kcn[:, kvh, c0:c0 + w])
                k.act(E[:, c0:c0 + w], psS[c0 // 512][:, 0:w], AF.Exp, scale=SCALE)
            k.tt(E[:, lo:ncols], E[:, lo:ncols], cmpM[:, ulo:ulo + (ncols - lo)], ALU.mult)
            if ncols < nch * 128:
                k.memset(E[:, ncols:nch * 128], 0.0)
            k.reduce(den[:, hh:hh + 1], E[:, 0:ncols], ALU.add)
            k.ts(den[:, hh:hh + 1], den[:, hh:hh + 1], 1e-30, ALU.max)
            k.recip(rden[:, hh:hh + 1], den[:, hh:hh + 1])
            if g == 0:
                k.ts(Pacc[:, 0:ncols], E[:, 0:ncols], rden[:, hh:hh + 1], ALU.mult)
            else:
                k.stt(Pacc[:, 0:ncols], E[:, 0:ncols], rden[:, hh:hh + 1], Pacc[:, 0:ncols], ALU.mult, ALU.add)
            pb = Pb[hh % 2]
            k.copy(pb[:, 0:nch * 128], E[:, 0:nch * 128], eng=ACT)
            for ch in range(nch):
                i = cnt["pb"] % 2
                cnt["pb"] += 1
                k.transpose(psT[:, i * 128:(i + 1) * 128], pb[:, ch * 128:(ch + 1) * 128], ident[:])
                k.copy(PbT[i][:], psT[:, i * 128:(i + 1) * 128], eng=ACT)
                k.mm(psA[hh // 4][:, (hh % 4) * 128:(hh % 4 + 1) * 128], PbT[i][:], Vc[:, ch, kvh, :],
                     start=(ch == 0), stop=(ch == nch - 1))
            k.tt(fsc[:, hh:hh + 1], rden[:, hh:hh + 1], ng[:, hh * 3:hh * 3 + 1], ALU.mult)
            k.ts(nacc[:, hh * 128:(hh + 1) * 128], psA[hh // 4][:, (hh % 4) * 128:(hh % 4 + 1) * 128], fsc[:, hh:hh + 1], ALU.mult)
            if g == 3:
                P4 = Pacc[:, 0:ncols].re("p (b f) -> p b f", f=4)
                iv = imp[:, 0:nblk]
                k.tt(iv, P4[:, :, 0], P4[:, :, 1], ALU.add)
                k.tt(iv, iv, P4[:, :, 2], ALU.add)
                k.stt(iv, iv, 2.0, P4[:, :, 3], ALU.mult, ALU.add)
                k.tt(imp[:, 1:nblk], imp[:, 1:nblk], P4[:, 0:nblk - 1, 3], ALU.add)
                wl = max(0, 16 * j - 1)
                wu = wl - (16 * j - 1)
                k.tt(imp[:, wl:nblk], imp[:, wl:nblk], FA[:, wu:17], ALU.add)
                k.ts(imp[:, 0:1], imp[:, 0:1], 1.0e4, ALU.add)
                k.op(DVE, lambda: k.nc.vector.max(out=mx.h[:, 0:8], in_=imp.h[:, 0:nblk]), [imp[:]], [mx[:]])
                k.op(DVE, lambda: k.nc.vector.match_replace(out=imp2.h[:, 0:nblk], in_to_replace=mx.h[:, 0:8],
                                                              in_values=imp.h[:, 0:nblk], imm_value=-3.0e4),
                     [imp[:], mx[:]], [imp2[:]])
                k.op(DVE, lambda: k.nc.vector.max(out=mx.h[:, 8:16], in_=imp2.h[:, 0:nblk]), [imp2[:]], [mx[:]])
                k.ts(selb[:, 0:nblk], imp[:, 0:nblk], mx[:, 15:16], ALU.is_ge)
                k.tt(selb[:, wl:nblk], selb[:, wl:nblk], VA[:, wu:17], ALU.mult)
                ngrp = (nblk + 127) // 128
                for gb in range(ngrp):
                    k.transpose(psT[:, 512 + gb * 128:512 + (gb + 1) * 128], selb[:, gb * 128:(gb + 1) * 128], ident[:])
                    k.copy(selT[gb][:], psT[:, 512 + gb * 128:512 + (gb + 1) * 128], eng=ACT)
                units = []
                for w in range(j + 1):
                    hold = {}

                    def pre(w=w, hold=hold):
                        kt_, vt_ = KT[cnt["kv"] % 2], Vt[cnt["kv"] % 2]
                        cnt["kv"] += 1
                        srck = agc_ap.rearrange("r c t -> c r t")[R_KS + kvh * 128:R_KS + (kvh + 1) * 128, :, w * 128:(w + 1) * 128]
                        k.dma(SP, kt_[:, 0:8, :], agc.raw(srck))
                        srcv = agv_ap.rearrange("r t c -> t r c")[w * 128:(w + 1) * 128, :, kvh * 128:(kvh + 1) * 128]
                        k.dma(QB, vt_[:, 0:8, 0:128], agv.raw(srcv))
                        hold["kt"], hold["vt"] = kt_, vt_
                    for m in range(8):
                        blk0 = 16 * w + 2 * m
                        gb, ob = blk0 // 128, blk0 % 128
                        diag = (w == j)

                        def mask_fn(p, e, u, gb=gb, ob=ob, m=m, diag=diag):
                            k.mm(psM[:, 0:128], Gx[:, ob * 64:ob * 64 + 128], selT[gb][:])
                            mt = MT[u % 2]
                            k.copy(mt[:], psM[:, 0:128], eng=ACT)
                            if diag:
                                k.tt(mt[:], mt[:], CMT[:, m, :], ALU.mult)
                            k.tt(p[:], e[:], V(mt.h[:].unsqueeze(1).to_broadcast([128, 4, 128]), [mt.buf]), ALU.mult)
                        units.append({"pre": pre if m == 0 else None,
                                      "kt": (lambda hold=hold, m=m: hold["kt"][:, m, :]),
                                      "vt": (lambda hold=hold, m=m: hold["vt"][:, m, :]),
                                      "mask": mask_fn})
                run_units(qt, kvh, units)
                finalize(kvh, 1, ng, False)
                kt_, vt_ = KT[cnt["kv"] % 2], Vt[cnt["kv"] % 2]
                cnt["kv"] += 1
                kview = agc_ap.rearrange("r c t -> c r t")[R_KW + kvh * 128:R_KW + (kvh + 1) * 128]
                vview = agv_ap.rearrange("r t c -> t r c")
                if j > 0:
                    k.dma(SP, kt_[:, 0:4, :], agc.raw(kview[:, 4:8, (j - 1) * 128:j * 128]))
                    k.dma(QB, vt_[:, 0:4, 0:128], agv.raw(vview[(j - 1) * 128:j * 128, 4:8, 256 + kvh * 128:256 + (kvh + 1) * 128]))
                k.dma(SP, kt_[:, 4:12, :], agc.raw(kview[:, :, j * 128:(j + 1) * 128]))
                k.dma(QB, vt_[:, 4:12, 0:128], agv.raw(vview[j * 128:(j + 1) * 128, :, 256 + kvh * 128:256 + (kvh + 1) * 128]))
                ms = list(range(12)) if j > 0 else list(range(4, 12))
                units = []
                for m in ms:
                    def mask_fn(p, e, u, m=m):
                        k.tt(p[:], e[:], V(WMT.h[:, m, :].unsqueeze(1).to_broadcast([128, 4, 128]), [WMT.buf]), ALU.mult)
                    units.append({"pre": None, "kt": (lambda kt_=kt_, m=m: kt_[:, m, :]),
                                  "vt": (lambda vt_=vt_, m=m: vt_[:, m, :]), "mask": mask_fn})
                run_units(qt, kvh, units)
                finalize(kvh, 2, ng, False)
        ob_ = nob[j % 2]
        k.copy(ob_[:], nacc[:], eng=ACT)
        k.dma(SP, sc["nsa_o"].p(j, (slice(j * 128, (j + 1) * 128), slice(None))), ob_[:])
    k.pop()


def phase_out(k, io, sc, out):
    k.push()
    ident = k.sbuf("o_ident", [128, 128], BF16)
    k.dma(POOL, ident[:], io["ident"][:])
    selI = k.sbuf("o_selI", [128, 8, 128], BF16)
    k.dma(POOL, selI[:], io["c_seli"].raw(io["c_seli"].h.ap().rearrange("p (r t) -> p r t", r=8)))
    Wo = k.sbuf("o_Wo", [128, 16, DM], BF16)
    wo_v = io["w_out"].h.ap().rearrange("(kc p) n -> p kc n", p=128)
    for q in range(4):
        k.dma(POOL, Wo[:, :, q * 512:(q + 1) * 512], io["w_out"].raw(wo_v[:, :, q * 512:(q + 1) * 512]))
    gg = k.sbuf("o_gg", [128, 256], F32)
    k.dma(SP, gg[:], io["gla_norm_g"][:].bc([128, 256]))
    gf = k.sbuf("o_gf", [128, DM], F32)
    k.dma(SP, gf[:], io["ffn_norm_g"][:].bc([128, DM]))
    Oall = k.sbuf("o_Oall", [128, 8, 8, 128], BF16)
    og = k.sbuf("o_og", [128, 1024], F32)
    junk = k.sbuf("o_junk", [128, DM], F32)
    gt = k.sbuf("o_gt", [128, 1024], F32)
    mix = k.sbuf("o_mix", [128, DM], BF16)
    mixT = k.sbuf("o_mixT", [128, 16, 128], BF16)
    xt = k.sbuf("o_xt", [128, DM], F32)
    ht = k.sbuf("o_ht", [128, DM], F32)
    mb = k.sbuf("o_mb", [128, DM], BF16)
    mTt = k.sbuf("o_mTt", [128, 16, 128], BF16)
    st4 = k.sbuf("o_st4", [128, 8], F32)
    st5 = k.sbuf("o_st5", [128, 8], F32)
    rs4 = k.sbuf("o_rs4", [128, 8], F32)
    psG = [k.psum(f"o_psG{i}", [128, 512], F32) for i in range(4)]
    psT = [k.psum(f"o_psT{i}", [128, 1024], BF16) for i in range(2)]
    gall = sc["gla_all"]
    gall_v = gall.base.rearrange("rr (jj r p) e -> rr jj p r e", jj=16, r=8)
    mT_v = sc["mT"].h.ap().rearrange("p (kc t) -> p kc t", kc=16)
    for j in range(NTL):
        rows = slice(j * 128, (j + 1) * 128)
        for rr in range(8):
            k.dma(SP if rr % 2 == 0 else QB, Oall[:, :, rr, :], gall.raw(gall_v[rr, j]))
        k.dma(SP, gt[:], sc["gout"].all((rows, slice(None))))
        k.dma(SP, mix[:, 1024:2048], sc["nsa_o"].all((rows, slice(None))))
        k.dma(QB, xt[:], io["x"].all((rows, slice(None))))
        for half in range(2):
            for r in range(8):
                k.mm(psG[half][:], selI[:, r, :], Oall[:, r, half * 4:(half + 1) * 4, :], start=(r == 0), stop=(r == 7))
            k.copy(og[:, half * 512:(half + 1) * 512], psG[half][:], eng=ACT)
        for h in range(4):
            hs = slice(h * 256, (h + 1) * 256)
            k.stt(junk[:, 0:256], og[:, hs], 1.0, og[:, hs], ALU.mult, ALU.mult, accum_out=st4[:, h:h + 1])
        k.ts(st5[:, 0:4], st4[:, 0:4], 1.0 / 256, ALU.mult, EPS, ALU.add)
        k.act(st5[:, 0:4], st5[:, 0:4], AF.Ln)
        k.act(rs4[:, 0:4], st5[:, 0:4], AF.Exp, scale=-0.5)
        for h in range(4):
            hs = slice(h * 256, (h + 1) * 256)
            k.stt(og[:, hs], og[:, hs], rs4[:, h:h + 1], gg[:], ALU.mult, ALU.mult)
        k.tt(mix[:, 0:1024], og[:], gt[:], ALU.mult)
        for half in range(2):
            for q in range(8):
                c = half * 8 + q
                k.transpose(psT[half][:, q * 128:(q + 1) * 128], mix[:, c * 128:(c + 1) * 128], ident[:])
            k.copy(mixT[:, half * 8:half * 8 + 8, :], psT[half][:].re("p (a b) -> p a b", a=8), eng=ACT)
        for cb in range(4):
            for kc in range(16):
                k.mm(psG[cb][:], mixT[:, kc, :], Wo[:, kc, cb * 512:(cb + 1) * 512], start=(kc == 0), stop=(kc == 15))
            k.tt(ht[:, cb * 512:(cb + 1) * 512], psG[cb][:], xt[:, cb * 512:(cb + 1) * 512], ALU.add)
        k.dma(SP, sc["h"].p(j, (rows, slice(None))), ht[:])
        k.stt(junk[:], ht[:], 1.0, ht[:], ALU.mult, ALU.mult, accum_out=st4[:, 4:5])
        k.ts(st5[:, 4:5], st4[:, 4:5], 1.0 / DM, ALU.mult, EPS, ALU.add)
        k.act(st5[:, 4:5], st5[:, 4:5], AF.Ln)
        k.act(rs4[:, 4:5], st5[:, 4:5], AF.Exp, scale=-0.5)
        k.stt(mb[:], ht[:], rs4[:, 4:5], gf[:], ALU.mult, ALU.mult)
        for half in range(2):
            for q in range(8):
                c = half * 8 + q
                k.transpose(psT[half][:, q * 128:(q + 1) * 128], mb[:, c * 128:(c + 1) * 128], ident[:])
            k.copy(mTt[:, half * 8:half * 8 + 8, :], psT[half][:].re("p (a b) -> p a b", a=8), eng=ACT)
        k.dma(SP, sc["mT"].raw(mT_v[:, :, j * 128:(j + 1) * 128], keys=[j]), mTt[:])
    k.pop()
    k.push()
    aT = k.sbuf("f_aT", [128, 44, 1024], BF16)
    psF = [k.psum(f"f_ps{i}", [128, 512], F32) for i in range(8)]
    wg_v = io["w_gate"].h.ap().rearrange("(kc p) n -> p kc n", p=128)
    wu_v = io["w_up"].h.ap().rearrange("(kc p) n -> p kc n", p=128)
    wd_v = io["w_down"].h.ap().rearrange("(fc p) n -> p fc n", p=128)
    for G in range(2):
        tok0 = G * 1024
        k.push()
        mTg = k.sbuf(f"f_mTg{G}", [128, 16, 1024], BF16)
        k.dma(SP, mTg[:], sc["mT"].raw(mT_v[:, :, tok0:tok0 + 1024], keys=list(range(8 * G, 8 * G + 8))))
        Wg = [k.sbuf(f"f_Wg{G}_{i}", [128, 16, 256], BF16) for i in range(2)]
        Wu = [k.sbuf(f"f_Wu{G}_{i}", [128, 16, 256], BF16) for i in range(2)]
        sg = [k.sbuf(f"f_sg{G}_{i}", [128, 512], F32) for i in range(2)]
        u = 0
        for fb in range(22):
            wg_, wu_ = Wg[fb % 2], Wu[fb % 2]
            k.dma(POOL, wg_[:], io["w_gate"].raw(wg_v[:, :, fb * 256:(fb + 1) * 256]))
            k.dma(POOL, wu_[:], io["w_up"].raw(wu_v[:, :, fb * 256:(fb + 1) * 256]))
            for sub in range(2):
                fc = fb * 2 + sub
                for tg in range(2):
                    pg, pu = psF[(u % 4) * 2], psF[(u % 4) * 2 + 1]
                    for kc in range(16):
                        k.mm(pg[:], wg_[:, kc, sub * 128:(sub + 1) * 128], mTg[:, kc, tg * 512:(tg + 1) * 512],
                             start=(kc == 0), stop=(kc == 15))
                    for kc in range(16):
                        k.mm(pu[:], wu_[:, kc, sub * 128:(sub + 1) * 128], mTg[:, kc, tg * 512:(tg + 1) * 512],
                             start=(kc == 0), stop=(kc == 15))
                    s_ = sg[u % 2]
                    k.act(s_[:], pg[:], AF.Silu)
                    k.tt(aT.p(("a", fc, tg), (slice(None), fc, slice(tg * 512, (tg + 1) * 512))), s_[:], pu[:], ALU.mult)
                    u += 1
        k.pop()
        k.push()
        Wd = [k.sbuf(f"f_Wd{G}_{i}", [128, 44, 256], BF16) for i in range(2)]
        hb = [k.sbuf(f"f_hb{G}_{i}", [128, 256], F32) for i in range(3)]
        yb = [k.sbuf(f"f_yb{G}_{i}", [128, 256], F32) for i in range(3)]
        u = 0
        for cb in range(8):
            wd_ = Wd[cb % 2]
            cs = slice(cb * 256, (cb + 1) * 256)
            for q in range(4):
                k.dma(POOL, wd_[:, q * 11:(q + 1) * 11, :], io["w_down"].raw(wd_v[:, q * 11:(q + 1) * 11, cs]))
            for tl in range(8):
                jt = 8 * G + tl
                rows = slice(jt * 128, (jt + 1) * 128)
                ps = psF[u % 8]
                h_, y_ = hb[u % 3], yb[u % 3]
                k.dma(SP, h_[:], sc["h"].all((rows, cs)))
                for fc in range(44):
                    k.mm(ps[:, 0:256], aT.ps([("a", fc, (tl * 128) // 512)], (slice(None), fc, slice(tl * 128, (tl + 1) * 128))),
                         wd_[:, fc, :], start=(fc == 0), stop=(fc == 43))
                k.tt(y_[:], ps[:, 0:256], h_[:], ALU.add)
                k.dma(QB if u % 2 else SP, out.p((jt, cb), (rows, cs)), y_[:])
                u += 1
        k.pop()
    k.pop()


INPUT_NAMES = ["x", "attn_norm_g", "w_in", "gla_conv_w", "gla_gate_w2", "gla_gate_b", "gla_norm_g",
               "nsa_q_norm_g", "nsa_kc_norm_g", "nsa_ks_norm_g", "nsa_kw_norm_g",
               "cmp_k_pos", "cmp_k_w1", "cmp_k_w2", "cmp_v_pos", "cmp_v_w1", "cmp_v_w2",
               "w_out", "ffn_norm_g", "w_gate", "w_up", "w_down"]

IN_SHAPES = {
    "x": [SL, DM], "attn_norm_g": [1, DM], "w_in": [DM, PW], "gla_norm_g": [1, 256],
    "nsa_q_norm_g": [1, 128], "nsa_kc_norm_g": [1, 128], "nsa_ks_norm_g": [1, 128], "nsa_kw_norm_g": [1, 128],
    "cmp_k_pos": [32, 128], "cmp_k_w1": [4096, 128], "cmp_k_w2": [128, 128],
    "cmp_v_pos": [32, 128], "cmp_v_w1": [4096, 128], "cmp_v_w2": [128, 128],
    "w_out": [DM, DM], "ffn_norm_g": [1, DM], "w_gate": [DM, DFF], "w_up": [DM, DFF], "w_down": [DFF, DM],
    "x_full": [GLA_SEGS * 512, DM], "gla_w": [DM, 400], "gla_cw": [128, 12], "gla_w2a": [17, 128],
    "ident": [128, 128], "c_tris": [128, 128], "c_m2": [128, 128], "c_gx": [128, 8192], "c_cmpm": [128, 128],
    "c_fa": [128, 17], "c_va": [128, 17], "c_cmt": [128, 8 * 128], "c_wmt": [128, 12 * 128], "c_seli": [128, 8 * 128],
}


def build_nc(debug=(), phases=("p1", "gla", "nsa", "out"), _declare=None):
    if _declare is None:
        dry = build_nc(debug, phases, _declare=())
        return build_nc(debug, phases, _declare=tuple(dry._used_inputs))
    nc = bass.Bass("TRN2", target_bir_lowering=False)
    es = contextlib.ExitStack()
    with es:
        k = K(nc, es)
        class _IO(dict):
            def __missing__(self, nm):
                self[nm] = k.dram(nm, IN_SHAPES[nm], F32, kind="ExternalInput")
                return self[nm]
        io = _IO()
        for nm in _declare:
            io[nm]
        out = k.dram("out", [SL, DM], F32, kind="ExternalOutput")
        agx_in = k.dram("agx_in", [3072, 2048], BF16)
        agx_out = k.dram("agx_out", [NCORES * 3072, 2048], BF16)
        fin = agx_in.h.ap().rearrange("r c -> (r c)")
        fout = agx_out.h.ap().rearrange("(rr r) c -> rr (r c)", rr=NCORES)
        E1, E2, E3 = 1024 * 2048, 1536 * 2048, 2560 * 2048
        sc = {
            "agx_in": agx_in, "agx_out": agx_out,
            "agc_in": TV(agx_in, fin[0:E1].rearrange("(c t) -> c t", t=2048), "agc_in"),
            "agv_in": TV(agx_in, fin[E1:E2].rearrange("(t c) -> t c", c=512), "agv_in"),
            "gla_in": TV(agx_in, fin[E2:E3].rearrange("(t e) -> t e", e=128), "gla_in"),
            "agc_out": TV(agx_out, fout[:, 0:E1].rearrange("rr (c t) -> rr c t", t=2048), "agc_out"),
            "agv_out": TV(agx_out, fout[:, E1:E2].rearrange("rr (t c) -> rr t c", c=512), "agv_out"),
            "gla_all": TV(agx_out, fout[:, E2:E3].rearrange("rr (t e) -> rr t e", e=128), "gla_all"),
            "gout": k.dram("gout", [SL, 1024], F32),
            "qn": k.dram("qn", [1024, SL], BF16),
            "ng": k.dram("ng", [SL, 24], F32),
            "nsa_o": k.dram("nsa_o", [SL, 1024], BF16),
            "h": k.dram("h", [SL, DM], F32),
            "mT": k.dram("mT", [128, 16 * SL], BF16),
        }
        if "noag" in phases:
            sc["_skip_ag"] = True
        if "p1" in phases:
            phase1(k, io, sc)
        if "gla" in phases:
            phase_gla(k, io, sc)
        if "nsa" in phases:
            phase_nsa(k, io, sc)
        if "out" in phases:
            phase_out(k, io, sc, out)
        for nm in debug:
            src = sc[nm]
            shp = list(src.h.shape)
            d = k.dram("dbg_" + nm, shp, src.h.dtype, kind="ExternalOutput")
            rows = shp[0]
            step = max(1, rows // 8)
            for r0 in range(0, rows, step):
                r1 = min(rows, r0 + step)
                k.dma(SP, d.p(r0, (slice(r0, r1), slice(None))), src.all((slice(r0, r1), slice(None))))
        k.barrier()
        nc._used_inputs = sorted(io.keys())
    return nc


def make_in_maps(inputs):
    f = lambda nm: np.asarray(inputs[nm], dtype=np.float32)[0]
    x = np.asarray(inputs["x"], dtype=np.float32).reshape(SEQ, DM)
    xt = x.reshape(NTL, NCORES, 128, DM)
    shared = {}
    for nm in IN_SHAPES:
        if nm in inputs and nm != "x":
            a = f(nm)
            if a.ndim == 1:
                a = a[None, :]
            shared[nm] = np.ascontiguousarray(a)
    w_in, conv, gw2, gb = f("w_in"), f("gla_conv_w"), f("gla_gate_w2"), f("gla_gate_b")
    p = np.arange(128)
    shared["ident"] = np.eye(128, dtype=np.float32)
    same = (p[:, None] // 64) == (p[None, :] // 64)
    m2 = (same & (p[:, None] <= p[None, :])).astype(np.float32)
    shared["c_m2"] = m2
    shared["c_tris"] = (m2 * (-1.0 / 16.0)).astype(np.float32)
    u = np.arange(8192)
    shared["c_gx"] = ((u[None, :] // 64) == p[:, None]).astype(np.float32)
    shared["x_full"] = np.ascontiguousarray(x)
    maps = []
    for c in range(NCORES):
        m = dict(shared)
        m["x"] = np.ascontiguousarray(xt[:, c].reshape(SL, DM))
        h, e = c // 2, c % 2
        qc = np.arange(h * 128, (h + 1) * 128)
        kc_ = 512 + qc
        vc_ = 1024 + h * 256 + e * 128 + np.arange(128)
        cols = np.concatenate([qc, kc_, vc_, np.arange(O_GLR, O_GLR + 16)])
        m["gla_w"] = np.ascontiguousarray(w_in[:, cols])
        m["gla_cw"] = np.ascontiguousarray(np.concatenate([conv[:, qc].T, conv[:, kc_].T, conv[:, vc_].T], axis=1))
        m["gla_w2a"] = np.ascontiguousarray(np.concatenate([gw2[:, h * 128:(h + 1) * 128], gb[None, h * 128:(h + 1) * 128]], axis=0))
        q = p[:, None]
        uu = np.arange(128)[None, :]
        m["c_cmpm"] = (16 * uu <= 128 * c + q + 993).astype(np.float32)
        u17 = np.arange(17)[None, :]
        jtp = 2 * c + 1 + (q >= 64)
        m["c_fa"] = (1.0e4 * ((u17 == jtp) | (u17 == jtp - 1)) - 1.0e4 * (u17 > jtp)).astype(np.float32)
        m["c_va"] = (u17 <= jtp).astype(np.float32)
        kp = p[:, None, None]
        mm_ = np.arange(8)[None, :, None]
        qq = p[None, None, :]
        m["c_cmt"] = ((128 * mm_ + kp) <= (128 * c + qq)).astype(np.float32).reshape(128, 8 * 128)
        m12 = np.arange(12)[None, :, None]
        d = 128 * (c + 4 - m12) + qq - kp
        m["c_wmt"] = ((d >= 0) & (d < 512)).astype(np.float32).reshape(128, 12 * 128)
        si = np.zeros((128, 8, 128), np.float32)
        si[:, c, :] = np.eye(128, dtype=np.float32)
        m["c_seli"] = si.reshape(128, 8 * 128)
        maps.append(m)
    return maps


_NC_CACHE = {}


def kernel(**inputs):
    if "nc" not in _NC_CACHE:
        _NC_CACHE["nc"] = build_nc()
    nc = _NC_CACHE["nc"]
    maps = make_in_maps(inputs)
    maps = [{nm: m[nm] for nm in nc._used_inputs} for m in maps]
    res = run_bass_kernel_spmd(nc, maps, core_ids=list(range(NCORES)))
    outs = [np.asarray(res.results[c]["out"], dtype=np.float32).reshape(NTL, 128, DM) for c in range(NCORES)]
    full = np.stack(outs, axis=1).reshape(1, SEQ, DM)
    return np.ascontiguousarray(full)
```
